# Optimizing a Trainium2 kernel written in Bass

```python
import math
import jax, jax.numpy as jnp
from jax import lax
import numpy as np

D_MODEL = 1024
BATCH = 1
SEQ = 16384
DEPTH = 4
DEC_BATCH = 8
DEC_SEQ = 16
PAST_LEN = 1024

CHUNK = 64
N_META = 16
N_MIXERS = 2
N_FOX = (DEPTH + 1) // 2
N_RET = DEPTH // 2
EPS = 1e-6

FOX_HEADS = 16
FOX_HEAD_DIM = 64
FOX_WIDTH = FOX_HEADS * FOX_HEAD_DIM
FOX_QBLOCK = 128
FOX_IN = 4 * FOX_WIDTH + FOX_HEADS

RET_HEADS = 4
RET_QK_DIM = 256
RET_V_DIM = 512
RET_QK_WIDTH = RET_HEADS * RET_QK_DIM
RET_V_WIDTH = RET_HEADS * RET_V_DIM
RET_IN = 2 * RET_QK_WIDTH + 2 * RET_V_WIDTH
ROPE_BASE = 10000.0

PEER_HEADS = 8
PEER_NKEYS = 128
PEER_EXPERTS = PEER_NKEYS * PEER_NKEYS
PEER_QDIM = 256
PEER_HALF = PEER_QDIM // 2
PEER_TOPK = 16
PEER_BLOCK = 128

kernel_name = 'fox_retention_peer_stream'


def _rmsnorm(x, g):
    xf = x.astype(jnp.float32)
    y = xf * lax.rsqrt(jnp.mean(xf * xf, axis=-1, keepdims=True) + EPS)
    return (y * g.astype(jnp.float32)).astype(x.dtype)


def _rotary(x, pos):
    half = x.shape[-1] // 2
    inv = ROPE_BASE ** (-jnp.arange(half, dtype=jnp.float32) / half)
    ang = pos.astype(jnp.float32)[:, None] * inv[None, :]
    cos = jnp.cos(ang)[:, None, :]
    sin = jnp.sin(ang)[:, None, :]
    xf = x.astype(jnp.float32)
    x1, x2 = xf[..., :half], xf[..., half:]
    return jnp.concatenate([x1 * cos - x2 * sin, x1 * sin + x2 * cos], axis=-1).astype(x.dtype)


def _ret_log_decay():
    return jnp.log(1.0 - 2.0 ** (-5.0 - jnp.arange(RET_HEADS, dtype=jnp.float32)))


def _fox_project(h, w_in, b_f, g_q, g_k):
    b, l, _ = h.shape
    z = h @ w_in
    shp = (b, l, FOX_HEADS, FOX_HEAD_DIM)
    q = _rmsnorm(z[..., :FOX_WIDTH].reshape(shp), g_q)
    k = _rmsnorm(z[..., FOX_WIDTH:2 * FOX_WIDTH].reshape(shp), g_k)
    v = z[..., 2 * FOX_WIDTH:3 * FOX_WIDTH].reshape(shp)
    og = z[..., 3 * FOX_WIDTH:4 * FOX_WIDTH]
    lf = jax.nn.log_sigmoid((z[..., 4 * FOX_WIDTH:] + b_f).astype(jnp.float32))
    return q, k, v, og, lf


def _fox_attend_prompt(q, k, v, lf):
    b, l, h, dh = q.shape
    c = jnp.cumsum(lf, axis=1)
    c_t = c.transpose(0, 2, 1)
    nb = -(-l // FOX_QBLOCK)
    pad = nb * FOX_QBLOCK - l
    qb = jnp.pad(q, ((0, 0), (0, pad), (0, 0), (0, 0))).reshape(b, nb, FOX_QBLOCK, h, dh).transpose(1, 0, 2, 3, 4)
    cb = jnp.pad(c, ((0, 0), (0, pad), (0, 0))).reshape(b, nb, FOX_QBLOCK, h).transpose(1, 0, 2, 3)
    kpos = jnp.arange(l)
    scale = dh ** -0.5

    def block(args):
        q_blk, c_blk, bi = args
        qpos = bi * FOX_QBLOCK + jnp.arange(FOX_QBLOCK)
        s = jnp.einsum('bqhd,bkhd->bhqk', q_blk, k).astype(jnp.float32) * scale
        s = s + (c_blk.transpose(0, 2, 1)[..., :, None] - c_t[:, :, None, :])
        s = jnp.where(kpos[None, :] <= qpos[:, None], s, -jnp.inf)
        p = jax.nn.softmax(s, axis=-1)
        return jnp.einsum('bhqk,bkhd->bqhd', p.astype(v.dtype), v)

    o = lax.map(block, (qb, cb, jnp.arange(nb)))
    return o.transpose(1, 0, 2, 3, 4).reshape(b, nb * FOX_QBLOCK, h, dh)[:, :l]


def _fox_attend_sample(q, k_new, v_new, lf_new, k_cache, v_cache, lf_cache):
    p_len = k_cache.shape[1]
    n = q.shape[1]
    k = jnp.concatenate([k_cache, k_new], axis=1)
    v = jnp.concatenate([v_cache, v_new], axis=1)
    c = jnp.cumsum(jnp.concatenate([lf_cache.astype(jnp.float32), lf_new], axis=1), axis=1)
    c_t = c.transpose(0, 2, 1)
    s = jnp.einsum('bqhd,bkhd->bhqk', q, k).astype(jnp.float32) * (q.shape[-1] ** -0.5)
    s = s + (c_t[:, :, p_len:, None] - c_t[:, :, None, :])
    mask = jnp.arange(p_len + n)[None, :] <= (p_len + jnp.arange(n))[:, None]
    s = jnp.where(mask, s, -jnp.inf)
    p = jax.nn.softmax(s, axis=-1)
    return jnp.einsum('bhqk,bkhd->bqhd', p.astype(v.dtype), v)


def _fox_output(o, og, w_out):
    b, l = o.shape[:2]
    y = o.reshape(b, l, FOX_WIDTH) * jax.nn.sigmoid(og.astype(jnp.float32)).astype(o.dtype)
    return y @ w_out


def _ret_project(h, w_in, pos):
    b, l, _ = h.shape
    z = h @ w_in
    q = _rotary(z[..., :RET_QK_WIDTH].reshape(b, l, RET_HEADS, RET_QK_DIM), pos) * (RET_QK_DIM ** -0.5)
    k = _rotary(z[..., RET_QK_WIDTH:2 * RET_QK_WIDTH].reshape(b, l, RET_HEADS, RET_QK_DIM), pos)
    v = z[..., 2 * RET_QK_WIDTH:2 * RET_QK_WIDTH + RET_V_WIDTH].reshape(b, l, RET_HEADS, RET_V_DIM)
    g = z[..., 2 * RET_QK_WIDTH + RET_V_WIDTH:]
    return q, k, v, g


def _retention_prompt(q, k, v):
    b, l, h, _ = q.shape
    lead = (-l) % CHUNK
    nc = (l + lead) // CHUNK
    padf = lambda a: jnp.pad(a.astype(jnp.float32), ((0, 0), (lead, 0), (0, 0), (0, 0)))
    qc = padf(q).reshape(b, nc, CHUNK, h, RET_QK_DIM)
    kc = padf(k).reshape(b, nc, CHUNK, h, RET_QK_DIM)
    vc = padf(v).reshape(b, nc, CHUNK, h, RET_V_DIM)
    logg = _ret_log_decay()
    idx = jnp.arange(CHUNK, dtype=jnp.float32)
    dmat = jnp.exp(logg[:, None, None] * jnp.abs(idx[:, None] - idx[None, :]))
    s = jnp.einsum('bnihd,bnjhd->bnhij', qc, kc) * dmat
    o_intra = jnp.einsum('bnhij,bnjhe->bnihe', s, vc)
    q_dec = jnp.exp(logg[None, :] * (idx[:, None] + 1.0))
    k_dec = jnp.exp(logg[None, :] * (CHUNK - 1.0 - idx[:, None]))
    c_dec = jnp.exp(logg * CHUNK)

    def step(state, inp):
        qn, kn, vn = inp
        o = jnp.einsum('bihd,bhde->bihe', qn, state) * q_dec[None, :, :, None]
        state = state * c_dec[None, :, None, None] + jnp.einsum('bjhd,jh,bjhe->bhde', kn, k_dec, vn)
        return state, o

    s0 = jnp.zeros((b, h, RET_QK_DIM, RET_V_DIM), jnp.float32)
    s_fin, o_inter = lax.scan(step, s0, (qc.transpose(1, 0, 2, 3, 4), kc.transpose(1, 0, 2, 3, 4), vc.transpose(1, 0, 2, 3, 4)))
    o = o_intra + o_inter.transpose(1, 0, 2, 3, 4)
    return o.reshape(b, nc * CHUNK, h, RET_V_DIM)[:, lead:], s_fin


def _retention_sample(q, k, v, state):
    n = q.shape[1]
    qf, kf, vf = q.astype(jnp.float32), k.astype(jnp.float32), v.astype(jnp.float32)
    st = state.astype(jnp.float32)
    logg = _ret_log_decay()
    idx = jnp.arange(n, dtype=jnp.float32)
    dmat = jnp.exp(logg[:, None, None] * jnp.abs(idx[:, None] - idx[None, :]))
    s = jnp.einsum('bihd,bjhd->bhij', qf, kf) * dmat
    o = jnp.einsum('bhij,bjhe->bihe', s, vf)
    o = o + jnp.einsum('bihd,bhde->bihe', qf, st) * jnp.exp(logg[None, :] * (idx[:, None] + 1.0))[None, :, :, None]
    k_dec = jnp.exp(logg[None, :] * (n - 1.0 - idx[:, None]))
    new_state = st * jnp.exp(logg * n)[None, :, None, None] + jnp.einsum('bjhd,jh,bjhe->bhde', kf, k_dec, vf)
    return o, new_state


def _ret_output(o, g, gn, w_out):
    b, l = o.shape[:2]
    of = o.astype(jnp.float32)
    mu = jnp.mean(of, axis=-1, keepdims=True)
    var = jnp.mean(jnp.square(of - mu), axis=-1, keepdims=True)
    y = ((of - mu) * lax.rsqrt(var + EPS)).reshape(b, l, RET_V_WIDTH) * gn.astype(jnp.float32)
    y = jax.nn.silu(g.astype(jnp.float32)) * y
    return y.astype(g.dtype) @ w_out


def _peer(h, w_q, subkeys, u_tab, v_tab):
    t, d = h.shape
    nb = -(-t // PEER_BLOCK)
    hb = jnp.pad(h, ((0, nb * PEER_BLOCK - t), (0, 0))).reshape(nb, PEER_BLOCK, d)

    def block(xb):
        qh = (xb @ w_q).reshape(PEER_BLOCK, PEER_HEADS, 2, PEER_HALF)
        s = jnp.einsum('thpc,hpkc->thpk', qh, subkeys).astype(jnp.float32)
        sv, si = lax.top_k(s, PEER_TOPK)
        cand = (sv[:, :, 0, :, None] + sv[:, :, 1, None, :]).reshape(PEER_BLOCK, PEER_HEADS, PEER_TOPK * PEER_TOPK)
        cid = (si[:, :, 0, :, None] * PEER_NKEYS + si[:, :, 1, None, :]).reshape(PEER_BLOCK, PEER_HEADS, PEER_TOPK * PEER_TOPK)
        fv, fi = lax.top_k(cand, PEER_TOPK)
        eid = jnp.take_along_axis(cid, fi, axis=-1)
        gate = jax.nn.softmax(fv, axis=-1)
        a = jax.nn.gelu(jnp.einsum('thkd,td->thk', u_tab[eid], xb).astype(jnp.float32), approximate=False)
        w = (gate * a).astype(xb.dtype)
        return jnp.einsum('thk,thkd->td', w, v_tab[eid])

    return lax.map(block, hb).reshape(nb * PEER_BLOCK, d)[:t]


def setup_inputs(seed: int = 0) -> dict:
    key = jax.random.key(seed)
    ks = jax.random.split(key, 24)
    f32 = jnp.float32
    nrm = lambda k, shape, scale: scale * jax.random.normal(k, shape, f32)
    return {
        'x_prompt': nrm(ks[0], (BATCH, SEQ, D_MODEL), 1.0),
        'x_sample': nrm(ks[1], (DEC_BATCH, DEC_SEQ, D_MODEL), 1.0),
        'cache_fox_k': nrm(ks[2], (N_FOX, DEC_BATCH, PAST_LEN, FOX_HEADS, FOX_HEAD_DIM), 1.0),
        'cache_fox_v': nrm(ks[3], (N_FOX, DEC_BATCH, PAST_LEN, FOX_HEADS, FOX_HEAD_DIM), 1.0),
        'cache_fox_lf': jax.nn.log_sigmoid(3.0 + nrm(ks[4], (N_FOX, DEC_BATCH, PAST_LEN, FOX_HEADS), 1.5)),
        'state_ret': nrm(ks[5], (N_RET, DEC_BATCH, RET_HEADS, RET_QK_DIM, RET_V_DIM), 0.5),
        'meta_tokens': nrm(ks[6], (N_META, D_MODEL), 1.0),
        'norm_mix': 1.0 + nrm(ks[7], (DEPTH, D_MODEL), 0.02),
        'norm_ffn': 1.0 + nrm(ks[8], (DEPTH, D_MODEL), 0.02),
        'fox_w_in': nrm(ks[9], (N_FOX, D_MODEL, FOX_IN), D_MODEL ** -0.5),
        'fox_b_f': jax.random.uniform(ks[10], (N_FOX, FOX_HEADS), f32, 1.0, 6.0),
        'fox_q_norm': 1.0 + nrm(ks[11], (N_FOX, FOX_HEAD_DIM), 0.02),
        'fox_k_norm': 1.0 + nrm(ks[12], (N_FOX, FOX_HEAD_DIM), 0.02),
        'fox_w_out': nrm(ks[13], (N_FOX, FOX_WIDTH, D_MODEL), 0.5 * FOX_WIDTH ** -0.5),
        'ret_w_in': nrm(ks[14], (N_RET, D_MODEL, RET_IN), D_MODEL ** -0.5),
        'ret_gn': 1.0 + nrm(ks[15], (N_RET, RET_V_WIDTH), 0.02),
        'ret_w_out': nrm(ks[16], (N_RET, RET_V_WIDTH, D_MODEL), 0.5 * RET_V_WIDTH ** -0.5),
        'peer_w_q': nrm(ks[17], (DEPTH, D_MODEL, PEER_HEADS * PEER_QDIM), D_MODEL ** -0.5),
        'peer_subkeys': nrm(ks[18], (DEPTH, PEER_HEADS, 2, PEER_NKEYS, PEER_HALF), PEER_HALF ** -0.5),
        'peer_u': nrm(ks[19], (DEPTH, PEER_EXPERTS, D_MODEL), D_MODEL ** -0.5),
        'peer_v': nrm(ks[20], (DEPTH, PEER_EXPERTS, D_MODEL), 0.1),
    }


def reference(x_prompt, x_sample, cache_fox_k, cache_fox_v, cache_fox_lf, state_ret,
              meta_tokens, norm_mix, norm_ffn, fox_w_in, fox_b_f, fox_q_norm, fox_k_norm, fox_w_out,
              ret_w_in, ret_gn, ret_w_out, peer_w_q, peer_subkeys, peer_u, peer_v):
    b, s_len, d = x_prompt.shape
    bd, n, _ = x_sample.shape
    p_len = cache_fox_k.shape[2]
    l = N_META + s_len
    xp = jnp.concatenate([jnp.broadcast_to(meta_tokens[None].astype(x_prompt.dtype), (b, N_META, d)), x_prompt], axis=1)
    xs = x_sample
    pos_p = jnp.arange(l) - N_META
    pos_s = p_len + jnp.arange(n)
    kp_l, vp_l, lfp_l, ks_l, vs_l, lfs_l, srp_l, srs_l = [], [], [], [], [], [], [], []
    for layer in range(DEPTH):
        j = layer // N_MIXERS
        hp = _rmsnorm(xp, norm_mix[layer])
        hs = _rmsnorm(xs, norm_mix[layer])
        if layer % N_MIXERS == 0:
            qp, kp, vp, gp, lfp = _fox_project(hp, fox_w_in[j], fox_b_f[j], fox_q_norm[j], fox_k_norm[j])
            qs, kss, vss, gs, lfs = _fox_project(hs, fox_w_in[j], fox_b_f[j], fox_q_norm[j], fox_k_norm[j])
            op = _fox_attend_prompt(qp, kp, vp, lfp)
            osm = _fox_attend_sample(qs, kss, vss, lfs, cache_fox_k[j], cache_fox_v[j], cache_fox_lf[j])
            xp = xp + _fox_output(op, gp, fox_w_out[j])
            xs = xs + _fox_output(osm, gs, fox_w_out[j])
            kp_l.append(kp)
            vp_l.append(vp)
            lfp_l.append(lfp.astype(x_prompt.dtype))
            ks_l.append(kss)
            vs_l.append(vss)
            lfs_l.append(lfs.astype(cache_fox_lf.dtype))
        else:
            qp, kp, vp, gp = _ret_project(hp, ret_w_in[j], pos_p)
            qs, kss, vss, gs = _ret_project(hs, ret_w_in[j], pos_s)
            op, st_p = _retention_prompt(qp, kp, vp)
            osm, st_s = _retention_sample(qs, kss, vss, state_ret[j])
            xp = xp + _ret_output(op, gp, ret_gn[j], ret_w_out[j])
            xs = xs + _ret_output(osm, gs, ret_gn[j], ret_w_out[j])
            srp_l.append(st_p.astype(x_prompt.dtype))
            srs_l.append(st_s.astype(state_ret.dtype))
        xp = xp + _peer(_rmsnorm(xp, norm_ffn[layer]).reshape(b * l, d), peer_w_q[layer], peer_subkeys[layer], peer_u[layer], peer_v[layer]).reshape(b, l, d)
        xs = xs + _peer(_rmsnorm(xs, norm_ffn[layer]).reshape(bd * n, d), peer_w_q[layer], peer_subkeys[layer], peer_u[layer], peer_v[layer]).reshape(bd, n, d)
    y_prompt = xp[:, N_META:]
    y_sample = xs
    new_fox_k_prompt = jnp.stack(kp_l)
    new_fox_v_prompt = jnp.stack(vp_l)
    new_fox_lf_prompt = jnp.stack(lfp_l)
    new_state_ret_prompt = jnp.stack(srp_l)
    new_fox_k_sample = jnp.stack(ks_l)
    new_fox_v_sample = jnp.stack(vs_l)
    new_fox_lf_sample = jnp.stack(lfs_l)
    new_state_ret_sample = jnp.stack(srs_l)
    return (y_prompt, y_sample, new_fox_k_prompt, new_fox_v_prompt, new_fox_lf_prompt, new_state_ret_prompt,
            new_fox_k_sample, new_fox_v_sample, new_fox_lf_sample, new_state_ret_sample)
```

```python
import contextlib
import math
import os

import numpy as np
import concourse.bass as bass
import concourse.mybir as mybir
from concourse.bass_utils import run_bass_kernel_spmd

F32 = mybir.dt.float32
BF16 = mybir.dt.bfloat16
U32 = mybir.dt.uint32
AF = mybir.ActivationFunctionType
ALU = mybir.AluOpType
AX = mybir.AxisListType

NCORES = 8
NT = 17
MISC = 16
D = 1024
EPS = 1e-6
NEG = -1.0e30
DEPTH = 4
FOX_IN = 4112
RET_IN = 6144
LOGG = [math.log(1.0 - 2.0 ** (-5.0 - h)) for h in range(4)]


class TK:
    def __init__(self, nc, es):
        self.nc, self.es = nc, es
        self.eng = {"pe": nc.tensor, "act": nc.scalar, "dve": nc.vector, "pool": nc.gpsimd, "sp": nc.sync}
        self.cur = {}
        self.streams = {}
        self.issued = {}
        self.sems = {}
        self.waited = {e: {} for e in self.eng}
        self.W = {}
        self.R = {}
        self.nsem = 0
        self.ninst = 0

    def _newsem(self, base):
        self.nsem += 1
        name = f"{base}_{self.nsem}"
        self.sems[name] = self.es.enter_context(self.nc.semaphore(name))
        return name

    def _tick(self, eng):
        c = self.cur.get(eng)
        if c is None:
            c = [self._newsem("e" + eng), 0]
            self.cur[eng] = c
        c[1] += 1
        return c[0], c[1]

    def _stick(self, stream):
        c = self.streams.get(stream)
        if c is None:
            c = [self._newsem("d"), 0]
            self.streams[stream] = c
        c[1] += 1
        self.issued[c[0]] = c[1] * 16
        return c[0], c[1] * 16

    @staticmethod
    def _add(deps, d):
        if d:
            for k, v in d.items():
                if deps.get(k, 0) < v:
                    deps[k] = v

    def op(self, eng, fn, reads=(), writes=(), stream=None, nowaw=False, inc=None):
        deps = {}
        for r in reads:
            self._add(deps, self.W.get(r))
        for w in writes:
            if not nowaw:
                self._add(deps, self.W.get(w))
            self._add(deps, self.R.get(w))
        E = self.eng[eng]
        own = self.cur[eng][0] if eng in self.cur else None
        wt = self.waited[eng]
        need = []
        for k, v in deps.items():
            if k in self.issued:
                v = self.issued[k]
            if eng == "pe" and stream is None and k == own:
                continue
            if wt.get(k, 0) >= v:
                continue
            need.append((k, v))
            wt[k] = v
        attach = None
        if need and stream is None and eng in ("pe", "act"):
            attach = need.pop()
        for k, v in need:
            E.wait_ge(self.sems[k], v)
        ins = fn()
        if attach is not None:
            ins._wait_ge(self.sems[attach[0]], attach[1])
        self.ninst += 1
        if stream is None:
            k, v = self._tick(eng)
            ins.then_inc(self.sems[k], 1)
        elif inc == 1:
            c = self.streams.get(stream)
            if c is None:
                c = [self._newsem("c"), 0]
                self.streams[stream] = c
            c[1] += 1
            k, v = c[0], c[1]
            ins.then_inc(self.sems[k])
        else:
            k, v = self._stick(stream)
            ins.then_inc(self.sems[k], 16)
        for w in writes:
            if nowaw and w in self.W:
                self.W[w][k] = max(self.W[w].get(k, 0), v)
            else:
                self.W[w] = {k: v}
            self.R[w] = {}
        for r in reads:
            d = self.R.setdefault(r, {})
            if d.get(k, 0) < v:
                d[k] = v
        return ins

    def barrier(self):
        allk = {}
        for e, c in self.cur.items():
            allk[c[0]] = c[1]
        for s, c in self.streams.items():
            allk[c[0]] = self.issued.get(c[0], c[1])
        for e, E in self.eng.items():
            wt = self.waited[e]
            for k, v in allk.items():
                if wt.get(k, 0) >= v:
                    continue
                E.wait_ge(self.sems[k], v)
                wt[k] = v


def build_program(stop=None):
    nc = bass.Bass("TRN2", target_bir_lowering=False)
    es = contextlib.ExitStack()
    uidc = [0]

    def uid():
        uidc[0] += 1
        return uidc[0]

    def din(name, shape, dt=F32):
        return nc.dram_tensor(name, list(shape), dt, kind="ExternalInput").ap()

    def dout(name, shape, dt=F32):
        return nc.dram_tensor(name, list(shape), dt, kind="ExternalOutput").ap()

    def dint(name, shape, dt):
        return nc.dram_tensor(name, list(shape), dt, kind="Internal").ap()

    xin = din("xin", [NT, 128, D])
    ck_in = din("ck", [2, 1024, 1024])
    cv_in = din("cv", [2, 1024, 1024])
    clf_in = din("clf", [2, 1024, 16])
    st_in = din("st", [2, 4, 256, 512])
    norm_mix = din("norm_mix", [DEPTH, D])
    norm_ffn = din("norm_ffn", [DEPTH, D])
    fox_w_in = din("fox_w_in", [2, D, FOX_IN])
    fox_b_f = din("fox_b_f", [2, 16])
    fox_q_norm = din("fox_q_norm", [2, 64])
    fox_k_norm = din("fox_k_norm", [2, 64])
    fox_w_out = din("fox_w_out", [2, D, D])
    ret_w_in = din("ret_w_in", [2, D, RET_IN])
    ret_gn = din("ret_gn", [2, 2048])
    ret_w_out = din("ret_w_out", [2, 2048, D])
    peer_w_q = din("peer_w_q", [DEPTH, D, 2048])
    peer_sk = din("peer_sk", [DEPTH, 16, 128, 128])
    peer_u_sh = din("peer_u_sh", [DEPTH * 16384 // NCORES, D])
    peer_v_sh = din("peer_v_sh", [DEPTH * 16384 // NCORES, D])
    c_rope = din("c_rope", [NT, 128, 256])
    c_maskj = din("c_maskj", [128, 8, 128])
    c_mc = din("c_mc", [128, 16])
    c_msel = din("c_msel", [128, 8])
    c_tri = din("c_tri", [128, 4, 128])
    c_dmask = din("c_dmask", [2, 128, 4, 128])
    c_qdec = din("c_qdec", [2, 128, 4, 128])
    c_kdec = din("c_kdec", [2, 128, 4])
    c_iota = din("c_iota", [128, 16])

    y_out = dout("y", [NT, 128, D])
    fk_out = dout("fk", [2, NT, 128, D])
    fv_out = dout("fv", [2, NT, 128, D])
    flf_out = dout("flf", [2, NT, 128, 16])
    srp_out = dout("srp", [2, 4, 256, 512])
    srs_out = dout("srs", [2, 4, 256, 512])

    xs = dint("xs", [NT, 128, D], F32)
    pu_loc = dint("pu_loc", [DEPTH * 16384 // NCORES, D], BF16)
    pv_loc = dint("pv_loc", [DEPTH * 16384 // NCORES, D], BF16)
    peer_u = dint("pu_all", [DEPTH * 16384, D], BF16)
    peer_v = dint("pv_all", [DEPTH * 16384, D], BF16)
    qts = dint("qts", [NT, 128, D], BF16)
    kts = dint("kts", [NT, 128, D], BF16)
    sgs = dint("sgs", [NT, 128, 2048], BF16)
    kv_loc = [dint(f"kv_loc{j}", [2048, 2048], BF16) for j in range(2)]
    kv_misc = [dint(f"kv_misc{j}", [128, 2048], BF16) for j in range(2)]
    kv_all = [dint(f"kv_all{j}", [NCORES * 2048, 2048], BF16) for j in range(2)]
    lf_loc = [dint(f"lf_loc{j}", [2048, 16], F32) for j in range(2)]
    lf_misc = [dint(f"lf_misc{j}", [128, 16], F32) for j in range(2)]
    lf_all = [dint(f"lf_all{j}", [NCORES * 2048, 16], F32) for j in range(2)]
    kd_loc = [dint(f"kd_loc{j}", [2048, 1024], BF16) for j in range(2)]
    kd_misc = [dint(f"kd_misc{j}", [128, 1024], BF16) for j in range(2)]
    kd_all = [dint(f"kd_all{j}", [NCORES * 2048, 1024], BF16) for j in range(2)]
    vr_loc = [dint(f"vr_loc{j}", [2048, 2048], BF16) for j in range(2)]
    vr_misc = [dint(f"vr_misc{j}", [128, 2048], BF16) for j in range(2)]
    vr_all = [dint(f"vr_all{j}", [NCORES * 2048, 2048], BF16) for j in range(2)]

    with es:
        T = TK(nc, es)
        V, A, PE, G, SP = nc.vector, nc.scalar, nc.tensor, nc.gpsimd, nc.sync

        def sbuf(st, name, shape, dt):
            return st.enter_context(nc.sbuf_tensor(f"{name}_{uid()}", list(shape), dt))

        def psum(st, name, shape, dt):
            return st.enter_context(nc.psum_tensor(f"{name}_{uid()}", list(shape), dt))

        def dma(out, in_, reads, writes, stream, eng="sp", nowaw=False):
            E = T.eng[eng]
            return T.op(eng, lambda: E.dma_start(out=out, in_=in_), reads, writes, stream=stream, nowaw=nowaw)

        def mm(out, lhsT, rhs, start, stop, reads, writes):
            return T.op("pe", lambda: PE.matmul(out, lhsT=lhsT, rhs=rhs, start=start, stop=stop), reads, writes)

        def tr(out, in_, ident, reads, writes):
            return T.op("pe", lambda: PE.transpose(out=out, in_=in_, identity=ident), reads, writes)

        def act(out, in_, func, reads, writes, bias=None, scale=None, accum_out=None):
            kw = {}
            if bias is not None:
                kw["bias"] = bias
            if scale is not None:
                kw["scale"] = scale
            if accum_out is not None:
                kw["accum_out"] = accum_out
            return T.op("act", lambda: A.activation(out=out, in_=in_, func=func, **kw), reads, writes)

        def acopy(out, in_, reads, writes):
            return T.op("act", lambda: A.copy(out=out, in_=in_), reads, writes)

        def vtt(out, in0, in1, op, reads, writes, eng="dve"):
            E = T.eng[eng]
            return T.op(eng, lambda: E.tensor_tensor(out=out, in0=in0, in1=in1, op=op), reads, writes)

        def vts(out, in0, s1, s2, op0, op1, reads, writes, eng="dve"):
            E = T.eng[eng]
            if s2 is None:
                return T.op(eng, lambda: E.tensor_single_scalar(out=out, in_=in0, scalar=s1, op=op0), reads, writes)
            return T.op(eng, lambda: E.tensor_scalar(out=out, in0=in0, scalar1=s1, scalar2=s2, op0=op0, op1=op1), reads, writes)

        def vstt(out, in0, scalar, in1, op0, op1, reads, writes, accum_out=None, eng="dve"):
            E = T.eng[eng]
            kw = {} if accum_out is None else {"accum_out": accum_out}
            return T.op(eng, lambda: E.scalar_tensor_tensor(out=out, in0=in0, scalar=scalar, in1=in1, op0=op0, op1=op1, **kw), reads, writes)

        def vcopy(out, in_, reads, writes, eng="dve"):
            E = T.eng[eng]
            return T.op(eng, lambda: E.tensor_copy(out=out, in_=in_), reads, writes)

        def vrecip(out, in_, reads, writes):
            return T.op("dve", lambda: V.reciprocal(out=out, in_=in_), reads, writes)

        def vreduce(out, in_, op, reads, writes):
            return T.op("dve", lambda: V.tensor_reduce(out=out, in_=in_, axis=AX.X, op=op), reads, writes)

        def vmemset(ap, val, writes, eng="dve"):
            E = T.eng[eng]
            return T.op(eng, lambda: E.memset(ap, val), (), writes)

        cst = es
        ident_bf = sbuf(cst, "identb", [128, 128], BF16)
        ident_f = sbuf(cst, "identf", [128, 128], F32)
        ones_f = sbuf(cst, "onesf", [128, 128], F32)
        tri = sbuf(cst, "tri", [128, 4, 128], F32)
        TRI_LE, TRI_GT, TRI_LT, TRI_MASK = (tri[:, i, :] for i in range(4))
        maskj = sbuf(cst, "maskj", [128, 8, 128], F32)
        mc = sbuf(cst, "mc", [128, 16], F32)
        msel = sbuf(cst, "msel", [128, 8], F32)
        iota16 = sbuf(cst, "iota", [128, 16], F32)
        vmemset(ident_f[:], 0.0, ["identf"])
        vmemset(ones_f[:], 1.0, ["onesf"])
        T.op("pool", lambda: G.affine_select(out=ident_f[:], in_=ident_f[:], pattern=[[-1, 128]], compare_op=ALU.not_equal,
                                             fill=1.0, base=0, channel_multiplier=1), ["identf"], ["identf"])
        vcopy(ident_bf[:], ident_f[:], ["identf"], ["identb"])
        dma(tri[:], c_tri, [], ["tri"], "cst")
        dma(maskj[:], c_maskj, [], ["maskj"], "cst")
        dma(mc[:], c_mc, [], ["mc"], "cst")
        dma(msel[:], c_msel, [], ["msel"], "cst")
        dma(iota16[:], c_iota, [], ["iota"], "cst")

        def load_w_cast(dst, src2d, kchunks, ncols, key):
            for kc in range(kchunks):
                for n0 in range(0, ncols, 2048):
                    n1 = min(ncols, n0 + 2048)
                    dma(dst[:, kc, n0:n1], src2d[kc * 128:(kc + 1) * 128, n0:n1], [], [key], "wload", eng="pool", nowaw=True)

        def load_bcast(dst, row_ap, key, n):
            dma(dst, row_ap.partition_broadcast(128), [], [key], "cst")

        def norm_transpose(W, xt, gain, tag):
            act(W["junk"][:], xt[:], AF.Square, ["xt"], ["junk", "ss"], accum_out=W["ss"][:, 0:1])
            act(W["ss"][:, 1:2], W["ss"][:, 0:1], AF.Sqrt, ["ss"], ["ss"], bias=EPS, scale=1.0 / D)
            vrecip(W["ss"][:, 2:3], W["ss"][:, 1:2], ["ss"], ["ss"])
            vstt(W["hn"][:], xt[:], W["ss"][:, 2:3], gain[:], ALU.mult, ALU.mult, ["xt", "ss", "gain"], ["hn"])
            if "hnf" in W:
                vstt(W["hnf"][:], xt[:], W["ss"][:, 2:3], gain[:], ALU.mult, ALU.mult, ["xt", "ss", "gain"], ["hnf"])
            for kc in range(8):
                tr(W["psT"][:, kc * 128:(kc + 1) * 128], W["hn"][:, kc * 128:(kc + 1) * 128], ident_bf[:], ["hn", "identb"], ["psT"])
            acopy(W["hT"][:], W["psT"][:], ["psT"], ["hT"])

        def transpose_to(W, src_bf, srckey, dst, dstkey, nchunks=8, pskey="psT"):
            for kc in range(nchunks):
                tr(W[pskey][:, kc * 128:(kc + 1) * 128], src_bf[:, kc * 128:(kc + 1) * 128], ident_bf[:], [srckey, "identb"], [pskey])
            acopy(dst, W[pskey][:, 0:nchunks * 128], [pskey], [dstkey])

        def fox_layer(L, j, xsrc):
            with contextlib.ExitStack() as st:
                Wb = sbuf(st, "w", [128, 8, FOX_IN], BF16)
                load_w_cast(Wb, fox_w_in[j], 8, FOX_IN, "wbig")
                gain = sbuf(st, "gain", [128, D], F32)
                load_bcast(gain[:], norm_mix[L], "gain", D)
                gq = sbuf(st, "gq", [128, 64], F32)
                gk = sbuf(st, "gk", [128, 64], F32)
                bfb = sbuf(st, "bfb", [128, 16], F32)
                load_bcast(gq[:], fox_q_norm[j], "gq", 64)
                load_bcast(gk[:], fox_k_norm[j], "gk", 64)
                load_bcast(bfb[:], fox_b_f[j], "bfb", 16)
                vts(gq[:], gq[:], 0.125, None, ALU.mult, None, ["gq"], ["gq"])
                W = dict(
                    junk=sbuf(st, "junk", [128, D], BF16), ss=sbuf(st, "ss", [128, 4], F32),
                    hn=sbuf(st, "hn", [128, D], BF16), hT=sbuf(st, "hT", [128, D], BF16),
                    psT=psum(st, "psT", [128, D], BF16), psT2=psum(st, "psT2", [128, D], BF16),
                )
                xt = sbuf(st, "xt", [128, D], F32)
                z = sbuf(st, "z", [128, FOX_IN], F32)
                tmp = sbuf(st, "tmp", [128, D], F32)
                ssq = sbuf(st, "ssq", [128, 48], F32)
                qn = sbuf(st, "qn", [128, D], BF16)
                kn = sbuf(st, "kn", [128, D], F32)
                knb = sbuf(st, "knb", [128, D], BF16)
                kvt = sbuf(st, "kvt", [128, 2048], BF16)
                sg = sbuf(st, "sg", [128, D], BF16)
                qT = sbuf(st, "qT", [128, D], BF16)
                lf = sbuf(st, "lf", [128, 64], F32)
                zps = [psum(st, f"zps{i}", [128, 512], F32) for i in range(2)]
                for t in range(NT):
                    dma(xt[:], xsrc[t], [], ["xt"], "xld")
                    norm_transpose(W, xt, gain, "f")
                    nblk = [(n0, min(n0 + 512, FOX_IN)) for n0 in range(0, FOX_IN, 512)]
                    for bi, (n0, n1) in enumerate(nblk):
                        pz = zps[bi % 2]
                        for kc in range(8):
                            mm(pz[:, 0:n1 - n0], W["hT"][:, kc * 128:(kc + 1) * 128], Wb[:, kc, n0:n1], kc == 0, kc == 7,
                               ["hT", "wbig"], [f"zps{bi % 2}"])
                        acopy(z[:, n0:n1], pz[:, 0:n1 - n0], [f"zps{bi % 2}"], ["z"])
                    for which, (c0, gg) in enumerate(((0, gq), (1024, gk))):
                        zz = z[:, c0:c0 + 1024]
                        vtt(tmp[:], zz, zz, ALU.mult, ["z"], ["tmp"])
                        vreduce(ssq[:, 0:16], tmp[:].rearrange("p (h e) -> p h e", e=64), ALU.add, ["tmp"], ["ssq"])
                        act(ssq[:, 16:32], ssq[:, 0:16], AF.Sqrt, ["ssq"], ["ssq"], bias=EPS, scale=1.0 / 64)
                        vrecip(ssq[:, 32:48], ssq[:, 16:32], ["ssq"], ["ssq"])
                        t3 = tmp[:].rearrange("p (h e) -> p h e", e=64)
                        vtt(t3, zz.rearrange("p (h e) -> p h e", e=64), ssq[:, 32:48].unsqueeze(2).to_broadcast([128, 16, 64]),
                            ALU.mult, ["z", "ssq"], ["tmp"])
                        dst = qn if which == 0 else kn
                        dkey = "qn" if which == 0 else "kn"
                        vtt(dst[:].rearrange("p (h e) -> p h e", e=64), t3, gg[:].unsqueeze(1).to_broadcast([128, 16, 64]),
                            ALU.mult, ["tmp", "gq", "gk"], [dkey])
                    acopy(knb[:], kn[:], ["kn"], ["knb"])
                    dma(fk_out[j, t], kn[:], ["kn"], [], "ost")
                    dma(fv_out[j, t], z[:, 2048:3072], ["z"], [], "ost")
                    vtt(lf[:, 0:16], z[:, 4096:4112], bfb[:], ALU.add, ["z", "bfb"], ["lf"])
                    act(lf[:, 16:32], lf[:, 0:16], AF.Exp, ["lf"], ["lf"], scale=-1.0)
                    act(lf[:, 32:48], lf[:, 16:32], AF.Ln, ["lf"], ["lf"], bias=1.0)
                    vts(lf[:, 48:64], lf[:, 32:48], -1.0, None, ALU.mult, None, ["lf"], ["lf"])
                    dma(flf_out[j, t], lf[:, 48:64], ["lf"], [], "ost")
                    dma(lf_loc[j][t * 128:(t + 1) * 128, :] if t < MISC else lf_misc[j], lf[:, 48:64], ["lf"], ["lf_loc"], "ost2", nowaw=True)
                    transpose_to(W, qn, "qn", qT[:], "qT")
                    dma(qts[t], qT[:], ["qT"], ["qts"], "ost2", nowaw=True)
                    transpose_to(W, knb, "knb", kvt[:, 0:1024], "kvt", pskey="psT2")
                    vcopy(kvt[:, 1024:2048], z[:, 2048:3072], ["z"], ["kvt"], eng="pool")
                    dma(kv_loc[j][t * 128:(t + 1) * 128, :] if t < MISC else kv_misc[j], kvt[:], ["kvt"], ["kv_loc"], "ost2", nowaw=True)
                    act(sg[:], z[:, 3072:4096], AF.Sigmoid, ["z"], ["sg"])
                    dma(sgs[t, :, 0:1024], sg[:], ["sg"], ["sgs"], "ost2", nowaw=True)
                T.barrier()
            if stop == f"F1_{L}":
                return False
            T.op("pool", lambda: G.collective_compute("AllGather", ALU.bypass, replica_groups=[list(range(NCORES))],
                                                      ins=[kv_loc[j].opt()], outs=[kv_all[j].opt()]),
                 ["kv_loc"], ["kv_all"], stream="cc", inc=1)
            T.op("pool", lambda: G.collective_compute("AllGather", ALU.bypass, replica_groups=[list(range(NCORES))],
                                                      ins=[lf_loc[j].opt()], outs=[lf_all[j].opt()]),
                 ["lf_loc"], ["lf_all"], stream="cc", inc=1)
            if stop == f"F2_{L}":
                T.barrier()
                return False
            with contextlib.ExitStack() as st:
                WO = sbuf(st, "wo", [128, 8, D], BF16)
                load_w_cast(WO, fox_w_out[j], 8, D, "wout")
                LF = sbuf(st, "LF", [128, 128, 16], F32)
                negC = sbuf(st, "negC", [128, 128, 16], F32)
                B2 = sbuf(st, "B2", [128, 128, 16], F32)
                totT = sbuf(st, "totT", [128, 16], F32)
                Eown = sbuf(st, "Eown", [128, 16, 16], F32)
                LFM = sbuf(st, "LFM", [128, 16], F32)
                BM = sbuf(st, "BM", [128, 16], F32)
                biasb = sbuf(st, "biasb", [128, 128, 16], F32)
                biasm = sbuf(st, "biasm", [128, 16], F32)
                PB = [psum(st, f"pb{i}", [128, 512], F32) for i in range(8)]
                LF4 = LF[:].rearrange("p (b r) h -> p b r h", r=8)
                for r in range(NCORES):
                    dma(LF4[:, :, r, :], lf_all[j][r * 2048:(r + 1) * 2048, :].rearrange("(b t) h -> t b h", t=128),
                        ["lf_all"], ["LF"], "ld0", nowaw=True)
                dma(LFM[:], lf_misc[j], ["lf_loc"], ["LFM"], "ld0")
                LF2 = LF[:].rearrange("p g h -> p (g h)")
                for q in range(4):
                    mm(PB[q][:], TRI_LE, LF2[:, q * 512:(q + 1) * 512], True, True, ["tri", "LF"], [f"pb{q}"])
                for h in range(16):
                    mm(PB[4][:, h:h + 1], LF[:, :, h], ones_f[:, 0:1], True, True, ["LF", "onesf"], ["pb4"])
                vcopy(totT[:], PB[4][:, 0:16], ["pb4"], ["totT"])
                vtt(B2[:], totT[:].unsqueeze(1).to_broadcast([128, 128, 16]), TRI_LT.unsqueeze(2).to_broadcast([128, 128, 16]),
                    ALU.mult, ["totT", "tri"], ["B2"])
                B22 = B2[:].rearrange("p g h -> p (g h)")
                negC2 = negC[:].rearrange("p g h -> p (g h)")
                for q in range(4):
                    mm(PB[5 + (q % 2)][:], ones_f[:], B22[:, q * 512:(q + 1) * 512], True, True, ["onesf", "B2"], [f"pb{5 + q % 2}"])
                    vcopy(biasb[:].rearrange("p g h -> p (g h)")[:, q * 512:(q + 1) * 512], PB[q][:], [f"pb{q}"], ["biasb"])
                    vstt(negC2[:, q * 512:(q + 1) * 512], biasb[:].rearrange("p g h -> p (g h)")[:, q * 512:(q + 1) * 512], -1.0,
                         PB[5 + (q % 2)][:], ALU.mult, ALU.subtract, ["biasb", f"pb{5 + q % 2}"], ["negC"])
                vtt(B2[:, 0:16, :], totT[:].unsqueeze(1).to_broadcast([128, 16, 16]), mc[:].unsqueeze(2).to_broadcast([128, 16, 16]),
                    ALU.mult, ["totT", "mc"], ["B2"])
                mm(PB[7][:, 0:256], ones_f[:], B22[:, 0:256], True, True, ["onesf", "B2"], ["pb7"])
                vcopy(Eown[:].rearrange("p b h -> p (b h)"), PB[7][:, 0:256], ["pb7"], ["Eown"])
                mm(PB[4][0:16, 16:32], TRI_GT[0:16, 0:16], LFM[0:16, :], True, True, ["tri", "LFM"], ["pb4"])
                vcopy(BM[0:16, :], PB[4][0:16, 16:32], ["pb4"], ["BM"])
                T.barrier()
                if stop == f"F3_{L}":
                    return False

                kvs = [sbuf(st, f"kvs{i}", [128, 2048], BF16) for i in range(3)]
                v65s = [sbuf(st, f"v65s{i}", [128, 16, 65], BF16) for i in range(3)]
                kvm = sbuf(st, "kvm", [128, 2048], BF16)
                v65m = sbuf(st, "v65m", [128, 16, 65], BF16)
                for i in range(3):
                    vmemset(v65s[i][:], 1.0, [f"v65s{i}"])
                vmemset(v65m[:], 1.0, ["v65m"])
                dma(kvm[:], kv_misc[j], ["kv_loc"], ["kvm"], "ld0")
                vcopy(v65m[:, :, 0:64], kvm[:, 1024:2048].rearrange("p (h e) -> p h e", e=64), ["kvm"], ["v65m"], eng="pool")
                qT = sbuf(st, "qT", [128, D], BF16)
                sg = sbuf(st, "sg", [128, D], BF16)
                xt = sbuf(st, "xt", [128, D], F32)
                pts = [sbuf(st, f"pt{i}", [128, 128], BF16) for i in range(4)]
                tms = [sbuf(st, f"tm{i}", [128, 128], F32) for i in range(2)]
                rec = sbuf(st, "rec", [128, 16], F32)
                o3 = sbuf(st, "o3", [128, 16, 64], F32)
                yb = sbuf(st, "yb", [128, D], BF16)
                yT = sbuf(st, "yT", [128, D], BF16)
                cnt = {"pt": 0, "tm": 0, "s": 0}
                PS_S = [0, 1, 2]
                PS_O = [3, 4, 5]


                def Oap(h, rows):
                    bk = PS_O[h // 7]
                    c0 = (h % 7) * 65
                    return PB[bk][rows[0]:rows[1], c0:c0 + 65], f"pb{bk}"

                def attend_block(ktile, kkey, vtile, vkey, kc0, nkc, krows, qc0, nq, bias_ap, bkey, maskap, first, last):
                    r0, r1 = krows
                    for h in range(16):
                        hp, e = divmod(h, 2)
                        si = PS_S[cnt["s"] % 3]
                        cnt["s"] += 1
                        ps = PB[si]
                        mm(ps[0:nkc, 0:nq], ktile[e * 64:(e + 1) * 64, hp * 128 + kc0:hp * 128 + kc0 + nkc],
                           qT[e * 64:(e + 1) * 64, hp * 128 + qc0:hp * 128 + qc0 + nq], True, True, [kkey, "qT"], [f"pb{si}"])
                        pt = pts[cnt["pt"] % 4]
                        pkey = f"pt{cnt['pt'] % 4}"
                        cnt["pt"] += 1
                        src, skey = ps, f"pb{si}"
                        if maskap is not None:
                            tm = tms[cnt["tm"] % 2]
                            tkey = f"tm{cnt['tm'] % 2}"
                            cnt["tm"] += 1
                            vtt(tm[r0:r1, 0:nq], ps[r0:r1, 0:nq], maskap, ALU.add, [skey, "maskj", "tri"], [tkey])
                            src, skey = tm, tkey
                        act(pt[r0:r1, 0:nq], src[r0:r1, 0:nq], AF.Exp, [skey, bkey], [pkey], bias=bias_ap(h))
                        oap, okey = Oap(h, (0, nq))
                        flush_pv()
                        pend.append((oap, pt[r0:r1, 0:nq], vtile[r0:r1, h, :], last, [pkey, vkey], [okey]))

                pend = []

                def flush_pv():
                    while pend:
                        oap, l_, r_, last_, rd, wr = pend.pop(0)
                        mm(oap, l_, r_, False, last_, rd, wr)

                def zero_O():
                    for bk in PS_O:
                        vmemset(PB[bk][:], 0.0, [f"pb{bk}"])

                def finish(rows, sgt, ykey="yb"):
                    flush_pv()
                    r0, r1 = rows
                    for bk, (h0, h1) in zip(PS_O, ((0, 7), (7, 14), (14, 16))):
                        nh = h1 - h0
                        pv = PB[bk][r0:r1, 0:nh * 65].rearrange("p (h e) -> p h e", e=65)
                        vrecip(rec[r0:r1, h0:h1], pv[:, :, 64], [f"pb{bk}"], ["rec"])
                        vtt(o3[r0:r1, h0:h1, :], pv[:, :, 0:64], rec[r0:r1, h0:h1].unsqueeze(2).to_broadcast([r1 - r0, nh, 64]),
                            ALU.mult, [f"pb{bk}", "rec"], ["o3"])
                    vtt(yb[r0:r1, :], o3[r0:r1].rearrange("p h e -> p (h e)"), sgt[r0:r1, :], ALU.mult, ["o3", "sg"], [ykey])

                def out_proj(t):
                    Wt = {"psT": PB[6][:].bitcast(BF16)}
                    for kc in range(8):
                        tr(PB[6][:].bitcast(BF16)[:, kc * 128:(kc + 1) * 128], yb[:, kc * 128:(kc + 1) * 128], ident_bf[:], ["yb", "identb"], ["pb6"])
                    acopy(yT[:], PB[6][:].bitcast(BF16), ["pb6"], ["yT"])
                    for nb in range(2):
                        for kc in range(8):
                            mm(PB[7][:], yT[:, kc * 128:(kc + 1) * 128], WO[:, kc, nb * 512:(nb + 1) * 512], kc == 0, kc == 7, ["yT", "wout"], ["pb7"])
                        vtt(xt[:, nb * 512:(nb + 1) * 512], xt[:, nb * 512:(nb + 1) * 512], PB[7][:], ALU.add, ["xt", "pb7"], ["xt"])
                    dma(xs[t], xt[:], ["xt"], ["xs"], "ost")

                nload = [0]
                for b in range(int(os.environ.get("MK_NB", "16"))):
                    dma(qT[:], qts[b], ["qts"], ["qT"], "ld1")
                    dma(sg[:], sgs[b, :, 0:1024], ["sgs"], ["sg"], "ld1")
                    dma(xt[:], xsrc[b], [], ["xt"], "xld")
                    nkb = 8 * b + 8
                    vtt(biasb[:, 0:nkb, :], Eown[:, b:b + 1, :].to_broadcast([128, nkb, 16]), negC[:, 0:nkb, :], ALU.add,
                        ["Eown", "negC"], ["biasb"])
                    vtt(biasm[0:16, :], Eown[0:16, b, :], BM[0:16, :], ALU.add, ["Eown", "BM"], ["biasm"])
                    zero_O()
                    attend_block(kvm, "kvm", v65m, "v65m", 0, 16, (0, 16), 0, 128, lambda h: biasm[0:16, h:h + 1], "biasm", None, True, False)
                    for kb in range(nkb):
                        slot = nload[0] % 3
                        nload[0] += 1
                        r, bl = kb % 8, kb // 8
                        row0 = (r * 16 + bl) * 128
                        dma(kvs[slot][:], kv_all[j][row0:row0 + 128, :], ["kv_all"], [f"kvs{slot}"], f"kv{slot}")
                        vcopy(v65s[slot][:, :, 0:64], kvs[slot][:, 1024:2048].rearrange("p (h e) -> p h e", e=64), [f"kvs{slot}"], [f"v65s{slot}"], eng="pool")
                        jj = kb - 8 * b
                        maskap = maskj[:, jj, :] if jj >= 0 else None
                        attend_block(kvs[slot], f"kvs{slot}", v65s[slot], f"v65s{slot}", 0, 128, (0, 128), 0, 128,
                                     (lambda h, kb=kb: biasb[:, kb, h:h + 1]), "biasb", maskap, False, kb == nkb - 1)
                    finish((0, 128), sg)
                    out_proj(b)
                if stop == f"F4a_{L}":
                    T.barrier()
                    return False

                dma(qT[:], qts[MISC], ["qts"], ["qT"], "ld1")
                dma(sg[:], sgs[MISC, :, 0:1024], ["sgs"], ["sg"], "ld1")
                dma(xt[:], xsrc[MISC], [], ["xt"], "xld")
                vmemset(yb[:], 0.0, ["yb"])
                zero_O()
                attend_block(kvm, "kvm", v65m, "v65m", 0, 16, (0, 16), 0, 16, lambda h: BM[0:16, h:h + 1], "BM", TRI_MASK[0:16, 0:16], True, True)
                finish((0, 16), sg)
                T.barrier()
                kc_st = sbuf(st, "kcst", [128, 1024], BF16)
                kvc = [sbuf(st, f"kvc{i}", [128, 1024], BF16) for i in range(8)]
                v65c = [sbuf(st, f"v65c{i}", [128, 16, 65], BF16) for i in range(8)]
                LFc = sbuf(st, "LFc", [128, 8, 16], F32)
                biasc = sbuf(st, "biasc", [128, 8, 16], F32)
                totc = sbuf(st, "totc", [128, 9, 16], F32)
                suf = sbuf(st, "suf", [128, 8, 16], F32)
                BN = sbuf(st, "BN", [128, 16], F32)
                vst = sbuf(st, "vst", [128, 1024], F32)
                dma(LFc[:], clf_in[j].rearrange("(kb t) h -> t kb h", t=128), [], ["LFc"], "ld0")
                for kb in range(8):
                    vmemset(v65c[kb][:], 1.0, [f"v65c{kb}"])
                    dma(kc_st[:], ck_in[j, kb * 128:(kb + 1) * 128, :], [], ["kcst"], "wload", eng="pool")
                    for kc in range(8):
                        tr(PB[6][:].bitcast(BF16)[:, kc * 128:(kc + 1) * 128], kc_st[:, kc * 128:(kc + 1) * 128], ident_bf[:], ["kcst", "identb"], ["pb6"])
                    acopy(kvc[kb][:], PB[6][:].bitcast(BF16), ["pb6"], [f"kvc{kb}"])
                    dma(vst[:], cv_in[j, kb * 128:(kb + 1) * 128, :], [], ["vst"], "ld1")
                    vcopy(v65c[kb][:, :, 0:64], vst[:].rearrange("p (h e) -> p h e", e=64), ["vst"], [f"v65c{kb}"])
                LFc2 = LFc[:].rearrange("p k h -> p (k h)")
                mm(PB[0][:, 0:128], TRI_GT, LFc2, True, True, ["tri", "LFc"], ["pb0"])
                mm(PB[1][:, 0:128], ones_f[:], LFc2, True, True, ["onesf", "LFc"], ["pb1"])
                mm(PB[1][:, 128:144], ones_f[32:48, :], LFM[32:48, :], True, True, ["onesf", "LFM"], ["pb1"])
                vcopy(totc[:].rearrange("p k h -> p (k h)"), PB[1][:, 0:144], ["pb1"], ["totc"])
                vcopy(suf[:, 7, :], totc[:, 8, :], ["totc"], ["suf"])
                for kb in range(6, -1, -1):
                    vtt(suf[:, kb, :], suf[:, kb + 1, :], totc[:, kb + 1, :], ALU.add, ["suf", "totc"], ["suf"])
                vtt(biasc[:].rearrange("p k h -> p (k h)"), suf[:].rearrange("p k h -> p (k h)"), PB[0][:, 0:128], ALU.add, ["suf", "pb0"], ["biasc"])
                mm(PB[2][0:48, 0:16], TRI_GT[32:48, 0:48], LFM[32:48, :], True, True, ["tri", "LFM"], ["pb2"])
                vcopy(BN[32:48, :], PB[2][32:48, 0:16], ["pb2"], ["BN"])
                T.barrier()
                zero_O()
                for kb in range(8):
                    attend_block(kvc[kb], f"kvc{kb}", v65c[kb], f"v65c{kb}", 0, 128, (0, 128), 0, 48, (lambda h, kb=kb: biasc[:, kb, h:h + 1]), "biasc", None, kb == 0, False)
                attend_block(kvm, "kvm", v65m, "v65m", 0, 48, (32, 48), 0, 48, lambda h: BN[32:48, h:h + 1], "BN", TRI_MASK[32:48, 0:48], False, True)
                finish((32, 48), sg)
                out_proj(MISC)
                T.barrier()
            return True

        def ret_layer(L, jr, xsrc):
            g128 = [math.exp(LOGG[h] * 128.0) for h in range(4)]
            g16 = [math.exp(LOGG[h] * 16.0) for h in range(4)]
            with contextlib.ExitStack() as st:
                Wb = sbuf(st, "w", [128, 8, RET_IN], BF16)
                load_w_cast(Wb, ret_w_in[jr], 8, RET_IN, "wbig")
                gain = sbuf(st, "gain", [128, D], F32)
                load_bcast(gain[:], norm_mix[L], "gain", D)
                kdc = sbuf(st, "kdc", [128, 2, 4], F32)
                dma(kdc[:], c_kdec.rearrange("a p h -> p a h"), [], ["kdc"], "cst")
                W = dict(
                    junk=sbuf(st, "junk", [128, D], BF16), ss=sbuf(st, "ss", [128, 4], F32),
                    hn=sbuf(st, "hn", [128, D], BF16), hT=sbuf(st, "hT", [128, D], BF16),
                    psT=psum(st, "psT", [128, D], BF16), psT2=psum(st, "psT2", [128, D], BF16),
                )
                xt = sbuf(st, "xt", [128, D], F32)
                z = sbuf(st, "z", [128, RET_IN], F32)
                rope = sbuf(st, "rope", [128, 256], F32)
                rope16 = sbuf(st, "rope16", [128, 256], F32)
                t1 = sbuf(st, "t1", [128, 4, 128], F32)
                t2 = sbuf(st, "t2", [128, 4, 128], F32)
                qr = sbuf(st, "qr", [128, D], BF16)
                kr = sbuf(st, "kr", [128, D], F32)
                krb = sbuf(st, "krb", [128, D], BF16)
                kdv = sbuf(st, "kdv", [128, 3072], BF16)
                sgb = sbuf(st, "sgb", [128, 2048], BF16)
                qT = sbuf(st, "qT", [128, D], BF16)
                kT = sbuf(st, "kT", [128, D], BF16)
                zps = [psum(st, f"zps{i}", [128, 512], F32) for i in range(2)]
                for t in range(NT):
                    dma(xt[:], xsrc[t], [], ["xt"], "xld")
                    dma(rope[:], c_rope[t], [], ["rope"], "ld1")
                    norm_transpose(W, xt, gain, "r")
                    for bi in range(RET_IN // 512):
                        n0 = bi * 512
                        pz = zps[bi % 2]
                        for kc in range(8):
                            mm(pz[:], W["hT"][:, kc * 128:(kc + 1) * 128], Wb[:, kc, n0:n0 + 512], kc == 0, kc == 7, ["hT", "wbig"], [f"zps{bi % 2}"])
                        acopy(z[:, n0:n0 + 512], pz[:], [f"zps{bi % 2}"], ["z"])
                    vts(rope16[:], rope[:], 1.0 / 16.0, None, ALU.mult, None, ["rope"], ["rope16"])
                    for which in range(2):
                        z4 = z[:, which * 1024:(which + 1) * 1024].rearrange("p (h two e) -> p h two e", two=2, e=128)
                        x1, x2 = z4[:, :, 0, :], z4[:, :, 1, :]
                        rp = rope16 if which == 0 else rope
                        rkey = "rope16" if which == 0 else "rope"
                        cosb = rp[:, 0:128].unsqueeze(1).to_broadcast([128, 4, 128])
                        sinb = rp[:, 128:256].unsqueeze(1).to_broadcast([128, 4, 128])
                        dst = qr if which == 0 else kr
                        dkey = "qr" if which == 0 else "kr"
                        d4 = dst[:].rearrange("p (h two e) -> p h two e", two=2, e=128)
                        vtt(t1[:], x1, cosb, ALU.mult, ["z", rkey], ["t1"])
                        vtt(t2[:], x2, sinb, ALU.mult, ["z", rkey], ["t2"])
                        vtt(d4[:, :, 0, :], t1[:], t2[:], ALU.subtract, ["t1", "t2"], [dkey])
                        vtt(t1[:], x1, sinb, ALU.mult, ["z", rkey], ["t1"])
                        vtt(t2[:], x2, cosb, ALU.mult, ["z", rkey], ["t2"])
                        vtt(d4[:, :, 1, :], t1[:], t2[:], ALU.add, ["t1", "t2"], [dkey])
                    acopy(krb[:], kr[:], ["kr"], ["krb"])
                    a = 1 if t == MISC else 0
                    vtt(kdv[:, 0:1024].rearrange("p (h e) -> p h e", e=256), kr[:].rearrange("p (h e) -> p h e", e=256),
                        kdc[:, a, :].unsqueeze(2).to_broadcast([128, 4, 256]), ALU.mult, ["kr", "kdc"], ["kdv"])
                    acopy(kdv[:, 1024:3072], z[:, 2048:4096], ["z"], ["kdv"])
                    dma(kd_loc[jr][t * 128:(t + 1) * 128, :] if t < MISC else kd_misc[jr], kdv[:, 0:1024], ["kdv"], ["kdv_loc"], "ost2", nowaw=True)
                    dma(vr_loc[jr][t * 128:(t + 1) * 128, :] if t < MISC else vr_misc[jr], kdv[:, 1024:3072], ["kdv"], ["kdv_loc"], "ost2", nowaw=True)
                    act(sgb[:], z[:, 4096:6144], AF.Silu, ["z"], ["sgb"])
                    dma(sgs[t], sgb[:], ["sgb"], ["sgs"], "ost2", nowaw=True)
                    transpose_to(W, qr, "qr", qT[:], "qT")
                    dma(qts[t], qT[:], ["qT"], ["qts"], "ost2", nowaw=True)
                    transpose_to(W, krb, "krb", kT[:], "kT", pskey="psT2")
                    dma(kts[t], kT[:], ["kT"], ["kts"], "ost2", nowaw=True)
                T.barrier()
            if stop == f"R1_{L}":
                return False
            T.op("pool", lambda: G.collective_compute("AllGather", ALU.bypass, replica_groups=[list(range(NCORES))],
                                                      ins=[kd_loc[jr].opt()], outs=[kd_all[jr].opt()]),
                 ["kdv_loc"], ["kdv_all"], stream="cc", inc=1)
            T.op("pool", lambda: G.collective_compute("AllGather", ALU.bypass, replica_groups=[list(range(NCORES))],
                                                      ins=[vr_loc[jr].opt()], outs=[vr_all[jr].opt()]),
                 ["kdv_loc"], ["kdv_all"], stream="cc", inc=1)
            with contextlib.ExitStack() as st:
                WO = sbuf(st, "wo", [128, 16, D], BF16)
                load_w_cast(WO, ret_w_out[jr], 16, D, "wout")
                gnb = sbuf(st, "gnb", [128, 2048], F32)
                load_bcast(gnb[:], ret_gn[jr], "gnb", 2048)
                dmk = sbuf(st, "dmk", [128, 2, 4, 128], F32)
                qdc = sbuf(st, "qdc", [128, 2, 4, 128], F32)
                dma(dmk[:], c_dmask.rearrange("a p h i -> p a h i"), [], ["dmk"], "cst")
                dma(qdc[:], c_qdec.rearrange("a p h i -> p a h i"), [], ["qdc"], "cst")
                S = sbuf(st, "S", [128, 8, 512], F32)
                Sown = [sbuf(st, f"Sown{i}", [128, 8, 512], BF16) for i in range(2)]
                kvs = [sbuf(st, f"kvs{i}", [128, 3072], BF16) for i in range(2)]
                kvm = sbuf(st, "kvm", [128, 3072], BF16)
                qT = sbuf(st, "qT", [128, D], BF16)
                qTd = sbuf(st, "qTd", [128, D], BF16)
                kT = sbuf(st, "kT", [128, D], BF16)
                vb = sbuf(st, "vb", [128, 2048], BF16)
                sgb = sbuf(st, "sgb", [128, 2048], BF16)
                xt = sbuf(st, "xt", [128, D], F32)
                pts = [sbuf(st, f"pt{i}", [128, 128], BF16) for i in range(2)]
                o = sbuf(st, "o", [128, 2048], F32)
                tmp = sbuf(st, "tmp", [128, 2048], F32)
                stt = sbuf(st, "stt", [128, 32], F32)
                yb = sbuf(st, "yb", [128, 2048], BF16)
                yT = sbuf(st, "yT", [128, 2048], BF16)
                PB = [psum(st, f"pb{i}", [128, 512], F32) for i in range(8)]
                PS_SC = [0, 1]
                PS_O = [2, 3, 4, 5]
                dma(kvm[:, 0:1024], kd_misc[jr], ["kdv_loc"], ["kvm"], "ld0", nowaw=True)
                dma(kvm[:, 1024:3072], vr_misc[jr], ["kdv_loc"], ["kvm"], "ld0", nowaw=True)

                def ret_out(t, a, Sb, skey, vtile, vkey, vrow0):
                    dma(qT[:], qts[t], ["qts"], ["qT"], "ld1")
                    dma(kT[:], kts[t], ["kts"], ["kT"], "ld1")
                    dma(sgb[:], sgs[t], ["sgs"], ["sgb"], "ld1")
                    dma(xt[:], xsrc[t], [], ["xt"], "xld")
                    vtt(qTd[:].rearrange("p (h c i) -> p h c i", c=2, i=128), qT[:].rearrange("p (h c i) -> p h c i", c=2, i=128),
                        qdc[:, a, :, :].unsqueeze(2).to_broadcast([128, 4, 2, 128]), ALU.mult, ["qT", "qdc"], ["qTd"])
                    for h in range(4):
                        for dc in range(2):
                            c0 = (h * 2 + dc) * 128
                            mm(PB[6][:, 0:128], kT[:, c0:c0 + 128], qT[:, c0:c0 + 128], dc == 0, dc == 1, ["kT", "qT"], ["pb6"])
                        pt = pts[h % 2]
                        vtt(pt[:], PB[6][:, 0:128], dmk[:, a, h, :], ALU.mult, ["pb6", "dmk"], [f"pt{h % 2}"])
                        ob = PS_O[h]
                        mm(PB[ob][:], pt[:], vtile[:, vrow0 + h * 512:vrow0 + (h + 1) * 512], True, False, [f"pt{h % 2}", vkey], [f"pb{ob}"])
                        for dc in range(2):
                            c0 = (h * 2 + dc) * 128
                            mm(PB[ob][:], qTd[:, c0:c0 + 128], Sb[:, h * 2 + dc, :], False, dc == 1, ["qTd", skey], [f"pb{ob}"])
                        acopy(o[:, h * 512:(h + 1) * 512], PB[ob][:], [f"pb{ob}"], ["o"])
                    o3 = o[:].rearrange("p (h e) -> p h e", e=512)
                    vreduce(stt[:, 0:4], o3, ALU.add, ["o"], ["stt"])
                    vtt(tmp[:], o[:], o[:], ALU.mult, ["o"], ["tmp"])
                    vreduce(stt[:, 4:8], tmp[:].rearrange("p (h e) -> p h e", e=512), ALU.add, ["tmp"], ["stt"])
                    vts(stt[:, 8:12], stt[:, 0:4], 1.0 / 512, None, ALU.mult, None, ["stt"], ["stt"])
                    vtt(stt[:, 12:16], stt[:, 8:12], stt[:, 8:12], ALU.mult, ["stt"], ["stt"])
                    vstt(stt[:, 16:20], stt[:, 4:8], 1.0 / 512, stt[:, 12:16], ALU.mult, ALU.subtract, ["stt"], ["stt"])
                    act(stt[:, 20:24], stt[:, 16:20], AF.Sqrt, ["stt"], ["stt"], bias=EPS, scale=1.0)
                    vrecip(stt[:, 24:28], stt[:, 20:24], ["stt"], ["stt"])
                    t3 = tmp[:].rearrange("p (h e) -> p h e", e=512)
                    vtt(t3, o3, stt[:, 8:12].unsqueeze(2).to_broadcast([128, 4, 512]), ALU.subtract, ["o", "stt"], ["tmp"])
                    vtt(t3, t3, stt[:, 24:28].unsqueeze(2).to_broadcast([128, 4, 512]), ALU.mult, ["tmp", "stt"], ["tmp"])
                    vtt(tmp[:], tmp[:], gnb[:], ALU.mult, ["tmp", "gnb"], ["tmp"])
                    vtt(yb[:], tmp[:], sgb[:], ALU.mult, ["tmp", "sgb"], ["yb"])
                    for half in range(2):
                        for kc in range(8):
                            tr(PB[6][:].bitcast(BF16)[:, kc * 128:(kc + 1) * 128], yb[:, (half * 8 + kc) * 128:(half * 8 + kc + 1) * 128],
                               ident_bf[:], ["yb", "identb"], ["pb6"])
                        acopy(yT[:, half * 1024:(half + 1) * 1024], PB[6][:].bitcast(BF16), ["pb6"], ["yT"])
                    for nb in range(2):
                        for kc in range(16):
                            mm(PB[7][:], yT[:, kc * 128:(kc + 1) * 128], WO[:, kc, nb * 512:(nb + 1) * 512], kc == 0, kc == 15, ["yT", "wout"], ["pb7"])
                        vtt(xt[:, nb * 512:(nb + 1) * 512], xt[:, nb * 512:(nb + 1) * 512], PB[7][:], ALU.add, ["xt", "pb7"], ["xt"])
                    dma(xs[t], xt[:], ["xt"], ["xs"], "ost")

                nsc = [0]
                for h in range(4):
                    for dc in range(2):
                        pi = PS_SC[nsc[0] % 2]
                        nsc[0] += 1
                        mm(PB[pi][:], kvm[0:16, h * 256 + dc * 128:h * 256 + (dc + 1) * 128], kvm[0:16, 1024 + h * 512:1024 + (h + 1) * 512],
                           True, True, ["kvm"], [f"pb{pi}"])
                        vcopy(S[:, h * 2 + dc, :], PB[pi][:], [f"pb{pi}"], ["S"])
                S2 = S[:].rearrange("p a e -> p (a e)")
                for g in range(128):
                    b, jj = g // 8, g % 8
                    so = Sown[b % 2]
                    sok = f"Sown{b % 2}"
                    so2 = so[:].rearrange("p a e -> p (a e)")
                    if jj == 0:
                        vts(so2, S2, msel[:, 0:1], None, ALU.mult, None, ["S", "msel"], [sok])
                    else:
                        vstt(so2, S2, msel[:, jj:jj + 1], so2, ALU.mult, ALU.add, ["S", "msel", sok], [sok])
                    slot = g % 2
                    row0 = (jj * 16 + b) * 128
                    dma(kvs[slot][:, 0:1024], kd_all[jr][row0:row0 + 128, :], ["kdv_all"], [f"kvs{slot}"], f"kv{slot}", nowaw=True)
                    dma(kvs[slot][:, 1024:3072], vr_all[jr][row0:row0 + 128, :], ["kdv_all"], [f"kvs{slot}"], f"kv{slot}", nowaw=True)
                    for h in range(4):
                        for dc in range(2):
                            pi = PS_SC[nsc[0] % 2]
                            nsc[0] += 1
                            mm(PB[pi][:], kvs[slot][:, h * 256 + dc * 128:h * 256 + (dc + 1) * 128],
                               kvs[slot][:, 1024 + h * 512:1024 + (h + 1) * 512], True, True, [f"kvs{slot}"], [f"pb{pi}"])
                            vstt(S[:, h * 2 + dc, :], S[:, h * 2 + dc, :], g128[h], PB[pi][:], ALU.mult, ALU.add, ["S", f"pb{pi}"], ["S"])
                    if jj == 7:
                        dma(vb[:], vr_loc[jr][b * 128:(b + 1) * 128, :], ["kdv_loc"], ["vb"], "ld1")
                        ret_out(b, 0, so, sok, vb, "vb", 0)
                dma(srp_out[jr].rearrange("h (c p) e -> p h c e", p=128), S[:].rearrange("p (h c) e -> p h c e", c=2), ["S"], [], "ost")
                if stop == f"R3a_{L}":
                    T.barrier()
                    return False
                T.barrier()
                Sin = S
                dma(Sin[:].rearrange("p (h c) e -> p h c e", c=2), st_in[jr].rearrange("h (c p) e -> p h c e", p=128), [], ["S"], "ld0")
                vcopy(Sown[0][:].rearrange("p a e -> p (a e)"), S2, ["S"], ["Sown0"])
                ret_out(MISC, 1, Sown[0], "Sown0", kvm, "kvm", 1024)
                for h in range(4):
                    for dc in range(2):
                        pi = PS_SC[nsc[0] % 2]
                        nsc[0] += 1
                        mm(PB[pi][:], kvm[32:48, h * 256 + dc * 128:h * 256 + (dc + 1) * 128], kvm[32:48, 1024 + h * 512:1024 + (h + 1) * 512],
                           True, True, ["kvm"], [f"pb{pi}"])
                        vstt(S[:, h * 2 + dc, :], S[:, h * 2 + dc, :], g16[h], PB[pi][:], ALU.mult, ALU.add, ["S", f"pb{pi}"], ["S"])
                dma(srs_out[jr].rearrange("h (c p) e -> p h c e", p=128), S[:].rearrange("p (h c) e -> p h c e", c=2), ["S"], [], "ost")
                T.barrier()
            return True

        def peer_layer(L, xsrc, xdst):
            with contextlib.ExitStack() as st:
                Wq = sbuf(st, "wq", [128, 8, 2048], BF16)
                load_w_cast(Wq, peer_w_q[L], 8, 2048, "wbig")
                gain = sbuf(st, "gain", [128, D], F32)
                load_bcast(gain[:], norm_ffn[L], "gain", D)
                skf = sbuf(st, "skf", [128, 16, 128], F32)
                skT = sbuf(st, "skT", [128, 16, 128], BF16)
                PB = [psum(st, f"pb{i}", [128, 512], F32) for i in range(8)]
                dma(skf[:], peer_sk[L].rearrange("g i c -> i g c"), [], ["skf"], "ld0")
                for g in range(16):
                    tr(PB[g % 4][:, 0:128], skf[:, g, :], ident_f[:], ["skf", "identf"], [f"pb{g % 4}"])
                    acopy(skT[:, g, :], PB[g % 4][:, 0:128], [f"pb{g % 4}"], ["skT"])
                W = dict(
                    junk=sbuf(st, "junk", [128, D], BF16), ss=sbuf(st, "ss", [128, 4], F32),
                    hn=sbuf(st, "hn", [128, D], BF16), hT=sbuf(st, "hT", [128, D], BF16),
                    hnf=sbuf(st, "hnf", [128, D], F32), psT=PB[7][:].bitcast(BF16),
                )
                W["psT"] = _APWrap(PB[7][:].bitcast(BF16))
                xt = sbuf(st, "xt", [128, D], F32)
                qT = sbuf(st, "qT", [128, 16, 128], BF16)
                s_sb = sbuf(st, "s_sb", [128, 16, 128], F32)
                s2 = sbuf(st, "s2", [128, 16, 128], F32)
                sv = sbuf(st, "sv", [128, 16, 16], F32)
                si = sbuf(st, "si", [128, 16, 16], U32)
                sif = sbuf(st, "sif", [128, 16, 16], F32)
                cand = sbuf(st, "cand", [128, 8, 256], F32)
                cand2 = sbuf(st, "cand2", [128, 8, 256], F32)
                fv = sbuf(st, "fv", [128, 8, 16], F32)
                fi = sbuf(st, "fi", [128, 8, 16], U32)
                ab = sbuf(st, "ab", [128, 2, 128], U32)
                abf = sbuf(st, "abf", [128, 2, 128], F32)
                oh = sbuf(st, "oh", [128, 8, 16, 16], F32)
                IJ = sbuf(st, "IJ", [128, 2, 128], F32)
                eidf = sbuf(st, "eidf", [128, 128], F32)
                eid = sbuf(st, "eid", [128, 128], U32)
                gate = sbuf(st, "gate", [128, 8, 16], F32)
                gsum = sbuf(st, "gsum", [128, 16], F32)
                adot = sbuf(st, "adot", [128, 128], F32)
                wgt = sbuf(st, "wgt", [128, 128], F32)
                NSLOT = 12
                gs = [sbuf(st, f"gs{i}", [128, D], BF16) for i in range(NSLOT)]
                acc = sbuf(st, "acc", [128, D], F32)
                junkf = sbuf(st, "junkf", [128, D], BF16)
                ng = [0]
                for t in range(NT):
                    dma(xt[:], xsrc[t], [], ["xt"], "xld")
                    norm_transpose(W, xt, gain, "p")
                    for g in range(16):
                        pq = PB[g // 4]
                        for kc in range(8):
                            mm(pq[:, (g % 4) * 128:(g % 4 + 1) * 128], Wq[:, kc, g * 128:(g + 1) * 128], W["hT"][:, kc * 128:(kc + 1) * 128],
                               kc == 0, kc == 7, ["wbig", "hT"], [f"pb{g // 4}"])
                        if g % 4 == 3:
                            acopy(qT[:, g - 3:g + 1, :].rearrange("p g t -> p (g t)"), pq[:], [f"pb{g // 4}"], ["qT"])
                    for g in range(16):
                        ps = PB[4 + (g // 4) % 2]
                        mm(ps[:, (g % 4) * 128:(g % 4 + 1) * 128], qT[:, g, :], skT[:, g, :], True, True, ["qT", "skT"], [f"pb{4 + (g // 4) % 2}"])
                        if g % 4 == 3:
                            acopy(s_sb[:, g - 3:g + 1, :].rearrange("p g i -> p (g i)"), ps[:], [f"pb{4 + (g // 4) % 2}"], ["s_sb"])
                    for g in range(16):
                        T.op("dve", lambda g=g: V.max(out=sv[:, g, 0:8], in_=s_sb[:, g, :]), ["s_sb"], ["sv"])
                        T.op("dve", lambda g=g: V.max_index(out=si[:, g, 0:8], in_max=sv[:, g, 0:8], in_values=s_sb[:, g, :]), ["s_sb", "sv"], ["si"])
                        T.op("dve", lambda g=g: V.match_replace(out=s2[:, g, :], in_to_replace=sv[:, g, 0:8], in_values=s_sb[:, g, :], imm_value=NEG),
                             ["s_sb", "sv"], ["s2"])
                        T.op("dve", lambda g=g: V.max(out=sv[:, g, 8:16], in_=s2[:, g, :]), ["s2"], ["sv"])
                        T.op("dve", lambda g=g: V.max_index(out=si[:, g, 8:16], in_max=sv[:, g, 8:16], in_values=s2[:, g, :]), ["s2", "sv"], ["si"])
                    vcopy(sif[:], si[:], ["si"], ["sif"])
                    sv4 = sv[:].rearrange("p (h two) k -> p h two k", two=2)
                    si4 = sif[:].rearrange("p (h two) k -> p h two k", two=2)
                    c4 = cand[:].rearrange("p h (a b) -> p h a b", b=16)
                    vtt(c4, sv4[:, :, 0, :].unsqueeze(3).to_broadcast([128, 8, 16, 16]), sv4[:, :, 1, :].unsqueeze(2).to_broadcast([128, 8, 16, 16]),
                        ALU.add, ["sv"], ["cand"])
                    for h in range(8):
                        T.op("dve", lambda h=h: V.max(out=fv[:, h, 0:8], in_=cand[:, h, :]), ["cand"], ["fv"])
                        T.op("dve", lambda h=h: V.max_index(out=fi[:, h, 0:8], in_max=fv[:, h, 0:8], in_values=cand[:, h, :]), ["cand", "fv"], ["fi"])
                        T.op("dve", lambda h=h: V.match_replace(out=cand2[:, h, :], in_to_replace=fv[:, h, 0:8], in_values=cand[:, h, :], imm_value=NEG),
                             ["cand", "fv"], ["cand2"])
                        T.op("dve", lambda h=h: V.max(out=fv[:, h, 8:16], in_=cand2[:, h, :]), ["cand2"], ["fv"])
                        T.op("dve", lambda h=h: V.max_index(out=fi[:, h, 8:16], in_max=fv[:, h, 8:16], in_values=cand2[:, h, :]), ["cand2", "fv"], ["fi"])
                    fi2 = fi[:].rearrange("p h k -> p (h k)")
                    vts(ab[:, 0, :], fi2, 4, None, ALU.logical_shift_right, None, ["fi"], ["ab"])
                    vts(ab[:, 1, :], fi2, 15, None, ALU.bitwise_and, None, ["fi"], ["ab"])
                    vcopy(abf[:], ab[:], ["ab"], ["abf"])
                    for w in range(2):
                        a4 = abf[:, w, :].rearrange("p (h k) -> p h k", k=16)
                        vtt(oh[:], a4.unsqueeze(3).to_broadcast([128, 8, 16, 16]),
                            iota16[:].unsqueeze(1).unsqueeze(1).to_broadcast([128, 8, 16, 16]), ALU.is_equal, ["abf", "iota"], ["oh"])
                        vtt(oh[:], oh[:], si4[:, :, w, :].unsqueeze(2).to_broadcast([128, 8, 16, 16]), ALU.mult, ["oh", "sif"], ["oh"])
                        vreduce(IJ[:, w, :].rearrange("p (h k) -> p h k", k=16), oh[:], ALU.add, ["oh"], ["IJ"])
                    vstt(eidf[:], IJ[:, 0, :], 128.0, IJ[:, 1, :], ALU.mult, ALU.add, ["IJ"], ["eidf"])
                    if L > 0:
                        vts(eidf[:], eidf[:], float(L * 16384), None, ALU.add, None, ["eidf"], ["eidf"])
                    vcopy(eid[:], eidf[:], ["eidf"], ["eid"])
                    vtt(gate[:], fv[:], fv[:, :, 0:1].to_broadcast([128, 8, 16]), ALU.subtract, ["fv"], ["gate"])
                    act(gate[:], gate[:], AF.Exp, ["gate"], ["gate"])
                    vreduce(gsum[:, 0:8], gate[:], ALU.add, ["gate"], ["gsum"])
                    vrecip(gsum[:, 8:16], gsum[:, 0:8], ["gsum"], ["gsum"])
                    vtt(gate[:], gate[:], gsum[:, 8:16].unsqueeze(2).to_broadcast([128, 8, 16]), ALU.mult, ["gate", "gsum"], ["gate"])
                    for m in range(128):
                        slot = ng[0] % NSLOT
                        ng[0] += 1
                        T.op("pool", lambda m=m, slot=slot: G.indirect_dma_start(
                            out=gs[slot][:], out_offset=None, in_=peer_u,
                            in_offset=bass.IndirectOffsetOnAxis(ap=eid[:, m:m + 1], axis=0)), ["eid", "pu_all"], [f"gs{slot}"], stream=f"g{slot}")
                        vstt(junkf[:], gs[slot][:], 1.0, W["hn"][:], ALU.mult, ALU.mult, [f"gs{slot}", "hn"], ["junkf", "adot"],
                             accum_out=adot[:, m:m + 1])
                    act(wgt[:], adot[:], AF.Gelu, ["adot"], ["wgt"])
                    vtt(wgt[:], wgt[:], gate[:].rearrange("p h k -> p (h k)"), ALU.mult, ["wgt", "gate"], ["wgt"])
                    vcopy(acc[:], xt[:], ["xt"], ["acc"])
                    for m in range(128):
                        slot = ng[0] % NSLOT
                        ng[0] += 1
                        T.op("pool", lambda m=m, slot=slot: G.indirect_dma_start(
                            out=gs[slot][:], out_offset=None, in_=peer_v,
                            in_offset=bass.IndirectOffsetOnAxis(ap=eid[:, m:m + 1], axis=0)), ["eid", "pv_all"], [f"gs{slot}"], stream=f"g{slot}")
                        vstt(acc[:], gs[slot][:], wgt[:, m:m + 1], acc[:], ALU.mult, ALU.add, [f"gs{slot}", "wgt", "acc"], ["acc"])
                    dma(xdst[t], acc[:], ["acc"], ["xs"], "ost")
                T.barrier()
            return True

        class _APWrap:
            def __init__(self, ap):
                self.ap = ap

            def __getitem__(self, k):
                return self.ap[k]

        def run_all():
            src = xin
            for nm, sh, loc, full in (("u", peer_u_sh, pu_loc, peer_u), ("v", peer_v_sh, pv_loc, peer_v)):
                rows = DEPTH * 16384 // NCORES
                with contextlib.ExitStack() as st:
                    cb = [sbuf(st, f"castb{i}", [128, 4, D], BF16) for i in range(2)]
                    for q in range(rows // 512):
                        r0 = q * 512
                        dma(cb[q % 2][:], sh[r0:r0 + 512, :].rearrange("(p r) d -> p r d", r=4), [], [f"castb{q % 2}"], f"tc{q % 2}", eng="pool")
                        dma(loc[r0:r0 + 512, :].rearrange("(p r) d -> p r d", r=4), cb[q % 2][:], [f"castb{q % 2}"], ["p%s_loc" % nm], "tb" + nm, nowaw=True)
                    T.barrier()
                T.op("pool", lambda loc=loc, full=full: G.collective_compute(
                    "AllGather", ALU.bypass, replica_groups=[list(range(NCORES))], ins=[loc.opt()], outs=[full.opt()]),
                    ["p%s_loc" % nm], ["p%s_all" % nm], stream="cc", inc=1)
            for L in range(DEPTH):
                if L % 2 == 0:
                    if not fox_layer(L, L // 2, src):
                        return
                else:
                    if not ret_layer(L, L // 2, src):
                        return
                src = xs
                if stop == f"M_{L}":
                    return
                dst = y_out if L == DEPTH - 1 else xs
                peer_layer(L, src, dst)
                if stop == f"P_{L}":
                    return

        run_all()
        T.barrier()
        if stop is not None:
            with contextlib.ExitStack() as st:
                dbg = sbuf(st, "dbg", [128, D], F32)
                for t in range(NT):
                    dma(dbg[:], xs[t], ["xs"], ["dbg"], "ld0")
                    dma(y_out[t], dbg[:], ["dbg"], [], "ost")
            T.barrier()
        build_program.ninst = T.ninst
    return nc


def _consts(c):
    f32 = np.float32
    half = 128
    inv = (np.float32(10000.0) ** (-np.arange(half, dtype=f32) / f32(half))).astype(f32)
    pos = np.zeros((NT, 128), f32)
    for b in range(16):
        pos[b] = (8 * b + c) * 128 + np.arange(128)
    pos[MISC, 0:16] = np.arange(16) - 16
    pos[MISC, 32:48] = 1024 + np.arange(16)
    ang = (pos[:, :, None] * inv[None, None, :]).astype(f32)
    rope = np.concatenate([np.cos(ang), np.sin(ang)], axis=-1).astype(f32)
    p = np.arange(128)[:, None]
    f = np.arange(128)[None, :]
    trimask = np.where(p <= f, 0.0, NEG).astype(f32)
    tri = np.stack([(p <= f).astype(f32), (p > f).astype(f32), (p < f).astype(f32), trimask], axis=1)
    maskj = np.zeros((128, 8, 128), f32)
    for j in range(8):
        if j == c:
            maskj[:, j, :] = trimask
        elif j > c:
            maskj[:, j, :] = NEG
    gp = np.arange(128)[:, None]
    mc = (gp <= 8 * np.arange(16)[None, :] + c).astype(f32)
    msel = np.zeros((128, 8), f32)
    msel[:, c] = 1.0
    logg = np.array(LOGG, dtype=np.float64)
    j_ = np.arange(128)[:, None]
    i_ = np.arange(128)[None, :]
    dm = np.zeros((2, 128, 4, 128), np.float64)
    qd = np.zeros((2, 128, 4, 128), np.float64)
    kd = np.zeros((2, 128, 4), np.float64)
    same = (j_ // 64) == (i_ // 64)
    cross = ((j_ // 64) == 0) & ((i_ // 64) == 1)
    in_m = (j_ < 16) & (i_ < 16)
    in_s = (j_ >= 32) & (j_ < 48) & (i_ >= 32) & (i_ < 48)
    for h in range(4):
        dm[0, :, h, :] = np.where(same, np.exp(logg[h] * np.abs(i_ - j_)), np.where(cross, np.exp(logg[h] * (i_ - j_)), 0.0))
        dm[1, :, h, :] = np.where(in_m | in_s, np.exp(logg[h] * np.abs(i_ - j_)), 0.0)
        qd[0, :, h, :] = np.exp(logg[h] * (np.arange(128) + 1.0))[None, :]
        qm = np.zeros(128)
        qm[32:48] = np.exp(logg[h] * (np.arange(16) + 1.0))
        qd[1, :, h, :] = qm[None, :]
        kd[0, :, h] = np.exp(logg[h] * (127.0 - np.arange(128)))
        km = np.zeros(128)
        km[0:16] = np.exp(logg[h] * (15.0 - np.arange(16)))
        km[32:48] = np.exp(logg[h] * (15.0 - np.arange(16)))
        kd[1, :, h] = km
    iota = np.tile(np.arange(16, dtype=f32)[None, :], (128, 1))
    return dict(c_rope=rope, c_maskj=maskj, c_mc=mc, c_msel=msel, c_tri=tri.astype(f32), c_dmask=dm.astype(f32),
                c_qdec=qd.astype(f32), c_kdec=kd.astype(f32), c_iota=iota)


_CACHE = {}


def kernel(x_prompt, x_sample, cache_fox_k, cache_fox_v, cache_fox_lf, state_ret, meta_tokens, norm_mix, norm_ffn,
           fox_w_in, fox_b_f, fox_q_norm, fox_k_norm, fox_w_out, ret_w_in, ret_gn, ret_w_out, peer_w_q, peer_subkeys,
           peer_u, peer_v):
    f32 = np.float32
    A = lambda a: np.ascontiguousarray(np.asarray(a), dtype=f32)
    stop = os.environ.get("MK_STOP")
    key = ("nc", stop, os.environ.get("MK_NB"), os.environ.get("MK_ROT"))
    if key not in _CACHE:
        _CACHE[key] = build_program(stop)
    nc = _CACHE[key]
    xp = A(x_prompt)[0].reshape(128, 128, D)
    xsm = A(x_sample)
    meta = A(meta_tokens)
    shared = dict(
        norm_mix=A(norm_mix), norm_ffn=A(norm_ffn), fox_w_in=A(fox_w_in), fox_b_f=A(fox_b_f), fox_q_norm=A(fox_q_norm),
        fox_k_norm=A(fox_k_norm), fox_w_out=A(fox_w_out), ret_w_in=A(ret_w_in), ret_gn=A(ret_gn), ret_w_out=A(ret_w_out),
        peer_w_q=A(peer_w_q), peer_sk=A(peer_subkeys).reshape(DEPTH, 16, 128, 128),
    )
    pu = A(peer_u).reshape(NCORES, DEPTH * 16384 // NCORES, D)
    pv = A(peer_v).reshape(NCORES, DEPTH * 16384 // NCORES, D)
    ck = A(cache_fox_k).reshape(2, 8, 1024, 1024)
    cv = A(cache_fox_v).reshape(2, 8, 1024, 1024)
    clf = A(cache_fox_lf)
    stt = A(state_ret)
    in_maps = []
    for c in range(NCORES):
        xin = np.zeros((NT, 128, D), f32)
        xin[0:16] = xp[c::8]
        xin[MISC, 0:16] = meta
        xin[MISC, 32:48] = xsm[c]
        m = dict(shared)
        m.update(xin=xin, ck=np.ascontiguousarray(ck[:, c]), cv=np.ascontiguousarray(cv[:, c]),
                 clf=np.ascontiguousarray(clf[:, c]), st=np.ascontiguousarray(stt[:, c]), peer_u_sh=pu[c], peer_v_sh=pv[c])
        m.update(_consts(c))
        in_maps.append(m)
    res = run_bass_kernel_spmd(nc, in_maps, core_ids=list(range(NCORES)))
    _CACHE["last"] = res
    R = res.results
    y_prompt = np.zeros((1, 16384, D), f32)
    yp = y_prompt[0].reshape(128, 128, D)
    y_sample = np.zeros((8, 16, D), f32)
    fk = np.zeros((2, 1, 16400, 1024), f32)
    fv = np.zeros((2, 1, 16400, 1024), f32)
    flf = np.zeros((2, 1, 16400, 16), f32)
    fks = np.zeros((2, 8, 16, 1024), f32)
    fvs = np.zeros((2, 8, 16, 1024), f32)
    flfs = np.zeros((2, 8, 16, 16), f32)
    srs = np.zeros((2, 8, 4, 256, 512), f32)
    for c in range(NCORES):
        r = R[c]
        yp[c::8] = r["y"][0:16]
        y_sample[c] = r["y"][MISC, 32:48]
        for nm, dst, dsts in (("fk", fk, fks), ("fv", fv, fvs), ("flf", flf, flfs)):
            w = dst.shape[-1]
            for jj in range(2):
                dst[jj, 0, 16:].reshape(128, 128, w)[c::8] = r[nm][jj, 0:16]
            dsts[:, c] = r[nm][:, MISC, 32:48]
            if c == 0:
                dst[:, 0, 0:16] = r[nm][:, MISC, 0:16]
        srs[:, c] = r["srs"]
    srp = R[0]["srp"][:, None].astype(f32)
    return (y_prompt, y_sample, fk.reshape(2, 1, 16400, 16, 64), fv.reshape(2, 1, 16400, 16, 64), flf.reshape(2, 1, 16400, 16),
            srp, fks.reshape(2, 8, 16, 16, 64), fvs.reshape(2, 8, 16, 16, 64), flfs, srs)
```
